# Optimizing a Trainium2 kernel written in Bass

```python
import jax
import jax.numpy as jnp
from jax import lax
import numpy as np

D_MODEL = 2048
BATCH = 1
SEQ = 8192
DEPTH = 2

MIX_WIDTH = D_MODEL
ATTN_WIDTH = MIX_WIDTH // 2
RWKV_WIDTH = MIX_WIDTH - ATTN_WIDTH
HEAD_DIM = 64
ATTN_HEADS = ATTN_WIDTH // HEAD_DIM
ATTN_KV_HEADS = 4
ATTN_GROUP = ATTN_HEADS // ATTN_KV_HEADS
KV_WIDTH = ATTN_KV_HEADS * HEAD_DIM
WINDOW = 128
ATTN_BLOCK = WINDOW
ROPE_THETA = 10000.0
RWKV_HEAD = 64
RWKV_HEADS = RWKV_WIDTH // RWKV_HEAD
DECAY_LORA = max(32, int(round(1.8 * RWKV_WIDTH ** 0.5 / 32)) * 32)
AAA_LORA = max(32, int(round(1.8 * RWKV_WIDTH ** 0.5 / 32)) * 32)
MV_LORA = max(32, int(round(1.3 * RWKV_WIDTH ** 0.5 / 32)) * 32)
GATE_LORA = max(32, int(round(0.6 * RWKV_WIDTH ** 0.8 / 32)) * 32)
RWKV_GN_EPS = 64e-5
FFN_HIDDEN = -(-8 * D_MODEL // (3 * 256)) * 256
LN_EPS = 1e-5
N_MOD = 6
DEEPNORM_ALPHA = (2.0 * DEPTH) ** 0.25
DEEPNORM_BETA = (8.0 * DEPTH) ** -0.25

ATTN_COLS = ATTN_WIDTH + 2 * KV_WIDTH
RWKV_SPLITS = (RWKV_WIDTH, 2 * RWKV_WIDTH, 3 * RWKV_WIDTH,
               3 * RWKV_WIDTH + DECAY_LORA, 3 * RWKV_WIDTH + DECAY_LORA + AAA_LORA)
RWKV_COLS = 3 * RWKV_WIDTH + DECAY_LORA + AAA_LORA + GATE_LORA
N_IN = ATTN_COLS + RWKV_COLS

kernel_name = "hymba_swa_sink_rwkv7_deepnorm_adaln"


def layer_norm(z, w, b):
    z32 = z.astype(jnp.float32)
    mu = jnp.mean(z32, axis=-1, keepdims=True)
    var = jnp.mean(jnp.square(z32 - mu), axis=-1, keepdims=True)
    return ((z32 - mu) * lax.rsqrt(var + LN_EPS) * w + b).astype(z.dtype)


def rope_tables(positions):
    inv = ROPE_THETA ** (-jnp.arange(0, HEAD_DIM, 2, dtype=jnp.float32) / HEAD_DIM)
    ang = positions.astype(jnp.float32)[..., None] * inv
    return jnp.cos(ang), jnp.sin(ang)


def apply_rope(t, cos, sin):
    t32 = t.astype(jnp.float32)
    t1, t2 = jnp.split(t32, 2, axis=-1)
    cos = cos[:, :, None, :]
    sin = sin[:, :, None, :]
    out = jnp.concatenate([t1 * cos - t2 * sin, t2 * cos + t1 * sin], axis=-1)
    return out.astype(t.dtype)


def window_mask(n_blocks):
    qa = jnp.arange(ATTN_BLOCK)[:, None]
    ks = jnp.arange(2 * ATTN_BLOCK)[None, :]
    rel = ATTN_BLOCK + qa - ks
    band = (rel >= 0) & (rel < WINDOW)
    blk = jnp.arange(n_blocks)[:, None, None]
    return band[None] & ((blk > 0) | (ks[None] >= ATTN_BLOCK))


def sliding_window_attention(q, k, v, sinks, cos, sin):
    B, T = q.shape[:2]
    nb = T // ATTN_BLOCK
    q = apply_rope(q.reshape(B, T, ATTN_HEADS, HEAD_DIM), cos, sin)
    k = apply_rope(k.reshape(B, T, ATTN_KV_HEADS, HEAD_DIM), cos, sin)
    v = v.reshape(B, T, ATTN_KV_HEADS, HEAD_DIM)
    qb = q.reshape(B, nb, ATTN_BLOCK, ATTN_KV_HEADS, ATTN_GROUP, HEAD_DIM)

    def with_prev(t):
        tb = t.reshape(B, nb, ATTN_BLOCK, ATTN_KV_HEADS, HEAD_DIM)
        prev = jnp.pad(tb[:, :-1], ((0, 0), (1, 0), (0, 0), (0, 0), (0, 0)))
        return jnp.concatenate([prev, tb], axis=2)

    kb, vb = with_prev(k), with_prev(v)
    s = jnp.einsum('bnqkgd,bnskd->bnkgqs', qb, kb,
                   preferred_element_type=jnp.float32) * (HEAD_DIM ** -0.5)
    mask = window_mask(nb)[None, :, None, None]
    s = jnp.where(mask, s, jnp.finfo(jnp.float32).min)
    sink = jnp.broadcast_to(
        sinks.astype(jnp.float32).reshape(1, 1, ATTN_KV_HEADS, ATTN_GROUP, 1, 1),
        s.shape[:-1] + (1,))
    p = jax.nn.softmax(jnp.concatenate([s, sink], axis=-1), axis=-1)[..., :-1]
    o = jnp.einsum('bnkgqs,bnskd->bnqkgd', p.astype(vb.dtype), vb)
    return o.reshape(B, T, ATTN_WIDTH)


def token_shift(z):
    return jnp.pad(z[:, :-1], ((0, 0), (1, 0), (0, 0)))


def wkv7_scan(r, decay, k, v, a, b):
    def step(S, inp):
        r_t, w_t, k_t, v_t, a_t, b_t = inp
        sa = jnp.einsum('bhij,bhj->bhi', S, a_t)
        S = (S * w_t[:, :, None, :] + sa[..., None] * b_t[:, :, None, :]
             + v_t[..., None] * k_t[:, :, None, :])
        return S, jnp.einsum('bhij,bhj->bhi', S, r_t)

    seq = tuple(jnp.moveaxis(t, 1, 0) for t in (r, decay, k, v, a, b))
    B = r.shape[0]
    S0 = jnp.zeros((B, RWKV_HEADS, RWKV_HEAD, RWKV_HEAD), jnp.float32)
    _, out = lax.scan(step, S0, seq)
    return jnp.moveaxis(out, 0, 1)


def rwkv7_group(cols, mu, w0, w2, a0, a2, g2, k_k, k_a, r_k, gn_w, gn_b, v_first, vres):
    B, T = cols.shape[:2]
    f32 = jnp.float32
    xs = cols + (token_shift(cols) - cols) * mu
    r, k, v, wd, ad, gd = jnp.split(xs, RWKV_SPLITS, axis=-1)
    w = -jax.nn.softplus(-(w0 + jnp.tanh(wd) @ w2)) - 0.5
    a = jax.nn.sigmoid(a0 + ad @ a2)
    g = jax.nn.sigmoid(gd) @ g2
    if vres is None:
        v_first = v
    else:
        v0, v1, v2 = vres
        v = v + (v_first - v) * jax.nn.sigmoid(v0 + (v @ v1) @ v2)

    def heads(t):
        return t.reshape(B, T, RWKV_HEADS, RWKV_HEAD).astype(f32)

    kk = heads(k * k_k)
    kk = kk / jnp.maximum(jnp.sqrt(jnp.sum(kk * kk, axis=-1, keepdims=True)), 1e-12)
    k = k * (1 + (a - 1) * k_a)
    rh, kh, vh, ah = heads(r), heads(k), heads(v), heads(a)
    decay = jnp.exp(-jnp.exp(heads(w)))
    o = wkv7_scan(rh, decay, kh, vh, -kk, kk * ah)
    mu_o = jnp.mean(o, axis=-1, keepdims=True)
    var_o = jnp.mean(jnp.square(o - mu_o), axis=-1, keepdims=True)
    o = ((o - mu_o) * lax.rsqrt(var_o + RWKV_GN_EPS) * gn_w.reshape(RWKV_HEADS, RWKV_HEAD)
         + gn_b.reshape(RWKV_HEADS, RWKV_HEAD))
    bonus = jnp.sum(rh * kh * r_k.astype(f32), axis=-1, keepdims=True) * vh
    o = (o + bonus).reshape(B, T, RWKV_WIDTH) * g
    return o.astype(cols.dtype), v_first


def setup_inputs(seed: int = 0) -> dict:
    key = jax.random.key(seed)
    ks = iter(jax.random.split(key, 40))
    nrm = lambda shape, s: jax.random.normal(next(ks), shape, jnp.float32) * s
    L, D, F = DEPTH, D_MODEL, FFN_HIDDEN
    start = jax.random.randint(next(ks), (BATCH, 1), 0, 4096, dtype=jnp.int32)
    positions = start + jnp.arange(SEQ, dtype=jnp.int32)[None, :]
    return {
        "x": nrm((BATCH, SEQ, D), 1.0),
        "c": nrm((BATCH, D), 1.0),
        "positions": positions,
        "w_ada": nrm((L, D, N_MOD * D), D ** -0.5),
        "b_ada": nrm((L, N_MOD * D), 0.02),
        "w_in": nrm((L, D, N_IN), D ** -0.5),
        "attn_sinks": nrm((L, ATTN_HEADS), 0.5),
        "rwkv_mu": jax.random.uniform(next(ks), (L, RWKV_COLS), jnp.float32),
        "rwkv_w0": jax.random.uniform(next(ks), (L, RWKV_WIDTH), jnp.float32, -6.5, -1.5),
        "rwkv_w2": nrm((L, DECAY_LORA, RWKV_WIDTH), 0.5 * DECAY_LORA ** -0.5),
        "rwkv_a0": nrm((L, RWKV_WIDTH), 0.1),
        "rwkv_a2": nrm((L, AAA_LORA, RWKV_WIDTH), 0.5 * AAA_LORA ** -0.5),
        "rwkv_g2": nrm((L, GATE_LORA, RWKV_WIDTH), GATE_LORA ** -0.5),
        "rwkv_k_k": 0.85 + nrm((L, RWKV_WIDTH), 0.02),
        "rwkv_k_a": 1.0 + nrm((L, RWKV_WIDTH), 0.02),
        "rwkv_r_k": nrm((L, RWKV_HEADS, RWKV_HEAD), 0.1),
        "rwkv_gn_w": 1.0 + nrm((L, RWKV_WIDTH), 0.02),
        "rwkv_gn_b": nrm((L, RWKV_WIDTH), 0.02),
        "rwkv_v0": 1.0 + nrm((L - 1, RWKV_WIDTH), 0.1),
        "rwkv_v1": nrm((L - 1, RWKV_WIDTH, MV_LORA), RWKV_WIDTH ** -0.5),
        "rwkv_v2": nrm((L - 1, MV_LORA, RWKV_WIDTH), 0.5 * MV_LORA ** -0.5),
        "w_out": nrm((L, MIX_WIDTH, D), DEEPNORM_BETA * MIX_WIDTH ** -0.5),
        "ln1_w": 1.0 + nrm((L, D), 0.02),
        "ln1_b": nrm((L, D), 0.02),
        "w_gate_up": nrm((L, D, 2 * F), D ** -0.5),
        "w_down": nrm((L, F, D), DEEPNORM_BETA * F ** -0.5),
        "ln2_w": 1.0 + nrm((L, D), 0.02),
        "ln2_b": nrm((L, D), 0.02),
    }


def reference(x, c, positions, w_ada, b_ada, w_in, attn_sinks, rwkv_mu, rwkv_w0, rwkv_w2,
              rwkv_a0, rwkv_a2, rwkv_g2, rwkv_k_k, rwkv_k_a, rwkv_r_k, rwkv_gn_w, rwkv_gn_b,
              rwkv_v0, rwkv_v1, rwkv_v2, w_out, ln1_w, ln1_b, w_gate_up, w_down, ln2_w, ln2_b):
    B = x.shape[0]
    cos, sin = rope_tables(positions)
    cond = jax.nn.silu(c)
    v_first = None
    for l in range(DEPTH):
        mod = (jnp.einsum('bd,de->be', cond, w_ada[l]) + b_ada[l]).reshape(B, N_MOD, 1, D_MODEL)
        sh_m, sc_m, gt_m, sh_f, sc_f, gt_f = [mod[:, i] for i in range(N_MOD)]

        u = x * (1 + sc_m) + sh_m
        proj = u @ w_in[l]
        q = proj[..., :ATTN_WIDTH]
        k = proj[..., ATTN_WIDTH:ATTN_WIDTH + KV_WIDTH]
        v = proj[..., ATTN_WIDTH + KV_WIDTH:ATTN_COLS]
        y_attn = sliding_window_attention(q, k, v, attn_sinks[l], cos, sin)
        vres = None if l == 0 else (rwkv_v0[l - 1], rwkv_v1[l - 1], rwkv_v2[l - 1])
        y_rwkv, v_first = rwkv7_group(
            proj[..., ATTN_COLS:], rwkv_mu[l], rwkv_w0[l], rwkv_w2[l], rwkv_a0[l],
            rwkv_a2[l], rwkv_g2[l], rwkv_k_k[l], rwkv_k_a[l], rwkv_r_k[l],
            rwkv_gn_w[l], rwkv_gn_b[l], v_first, vres)
        mix = jnp.concatenate([y_attn, y_rwkv], axis=-1) @ w_out[l]
        x = layer_norm(DEEPNORM_ALPHA * x + gt_m * mix, ln1_w[l], ln1_b[l])

        u = x * (1 + sc_f) + sh_f
        gate, up = jnp.split(u @ w_gate_up[l], 2, axis=-1)
        ffn = (jax.nn.silu(gate) * up) @ w_down[l]
        x = layer_norm(DEEPNORM_ALPHA * x + gt_f * ffn, ln2_w[l], ln2_b[l])
    return x
```

```python
import contextlib
import numpy as np
import ml_dtypes
import concourse.bass as bass
import concourse.mybir as mybir
from concourse.bass_utils import run_bass_kernel_spmd

F32 = mybir.dt.float32
BF16 = mybir.dt.bfloat16
I32 = mybir.dt.int32
AF = mybir.ActivationFunctionType
ALU = mybir.AluOpType
AX = mybir.AxisListType

ENGS = ("pe", "dve", "act", "pool", "sp")

D = 2048
KC = 16
TOK = 1024
NT = 8
NCORE = 8
FF = 5632
FFC = 44
NIN = 4896
RB = 1536
ALPHA = float((2.0 * 2) ** 0.25)
LN_EPS = 1e-5
GN_EPS = 64e-5
NEG_EXPHALF = -float(np.exp(-0.5))
TWO_PI_HI = 6.28125
TWO_PI_LO = float(2.0 * np.pi - 6.28125)


class Buf:
    __slots__ = ("name", "last_w", "readers")

    def __init__(self, name):
        self.name = name
        self.last_w = None
        self.readers = []


class Op:
    __slots__ = ("id", "eng", "fn", "waits", "signals", "is_dma", "sem", "val")

    def __init__(self, id, eng, fn, is_dma):
        self.id = id
        self.eng = eng
        self.fn = fn
        self.waits = []
        self.signals = False
        self.is_dma = is_dma
        self.sem = None
        self.val = None


class Tl:
    def __init__(self, ap, buf):
        self.ap = ap
        self.b = buf

    def __getitem__(self, idx):
        return self.ap[idx]


def _bufs(xs):
    out = []
    for x in xs:
        if x is None:
            continue
        out.append(x.b if isinstance(x, Tl) else x)
    return out


class Prog:
    N_DMA_SEMS = 24

    def __init__(self, arena_words=47 * 1024):
        self.nc = bass.Bass("TRN2", target_bir_lowering=False)
        self.ops = []
        self.stack = contextlib.ExitStack()
        self.waited = {e: {} for e in ENGS}
        self.last_op = {e: None for e in ENGS}
        self.dma_rr = {e: 0 for e in ENGS}
        self.dma_last = {}
        self.dma_cnt = {}
        self.unconsumed_dma = set()
        self.nbuf = 0
        self.arena = self.stack.enter_context(self.nc.sbuf_tensor("arena", [128, arena_words], F32))
        self.arena_words = arena_words
        self.top = 0
        self.hi_top = arena_words
        self.hi_mode = False
        self.nrot = 8
        self.psum_t = self.stack.enter_context(self.nc.psum_tensor("psum", [128, 8, 512], F32))
        self.banks = [Tl(self.psum_t[:, k, :], Buf(f"bank{k}")) for k in range(8)]
        self.bank_rr = 0
        self.n_dram = 0

    def alloc(self, shape, dtype, name=None, hi=None):
        n = int(np.prod(shape[1:]))
        words = n if dtype in (F32, I32) else (n + 1) // 2
        words = (words + 7) // 8 * 8
        hi = self.hi_mode if hi is None else hi
        assert self.top + words <= self.hi_top, (name, self.top, words, self.hi_top)
        if hi:
            self.hi_top -= words
            base = self.hi_top
        else:
            base = self.top
            self.top += words
        ap = self.arena[:, base:base + words]
        if dtype == BF16:
            ap = ap.bitcast(BF16)[:, 0:n]
        elif dtype == I32:
            ap = ap.bitcast(I32)[:, 0:n]
        else:
            ap = ap[:, 0:n]
        if len(shape) == 3:
            ap = ap.rearrange("p (a b) -> p a b", b=shape[2])
        elif len(shape) == 4:
            ap = ap.rearrange("p (a b c) -> p a b c", b=shape[2], c=shape[3])
        if shape[0] < 128:
            ap = ap[0:shape[0]]
        self.nbuf += 1
        return Tl(ap, Buf(name or f"t{self.nbuf}"))

    def mark(self):
        return (self.top, self.hi_top)

    def release(self, mark):
        self.barrier()
        self.top, self.hi_top = mark

    def bank(self):
        b = self.banks[self.bank_rr]
        self.bank_rr = (self.bank_rr + 1) % self.nrot
        return b

    def dram_in(self, name, shape, dtype=F32):
        return self.nc.dram_tensor(name, list(shape), dtype, kind="ExternalInput").ap()

    def dram_out(self, name, shape, dtype=F32):
        return self.nc.dram_tensor(name, list(shape), dtype, kind="ExternalOutput").ap()

    def _add_wait(self, op, dep):
        F = op.eng
        if dep.is_dma:
            key = ("d", dep.eng, dep.sem)
            if self.waited[F].get(key, 0) >= dep.val:
                return
            self.waited[F][key] = dep.val
            op.waits.append(dep)
            self.unconsumed_dma.discard(dep.id)
        else:
            key = ("e", dep.eng)
            if self.waited[F].get(key, -1) >= dep.id:
                return
            self.waited[F][key] = dep.id
            dep.signals = True
            op.waits.append(dep)

    def op(self, eng, fn, reads=(), writes=(), dma=False):
        reads = _bufs(reads)
        writes = _bufs(writes)
        o = Op(len(self.ops), eng, fn, dma)
        if dma:
            k = self.dma_rr[eng]
            self.dma_rr[eng] = (k + 1) % self.N_DMA_SEMS
            o.sem = k
            prev = self.dma_last.get((eng, k))
            cnt = self.dma_cnt.get((eng, k), 0) + 1
            self.dma_cnt[(eng, k)] = cnt
            o.val = 16 * cnt
            if prev is not None:
                self._add_wait(o, prev)
            self.dma_last[(eng, k)] = o
            self.unconsumed_dma.add(o.id)
        for r in reads:
            w = r.last_w
            if w is not None:
                self._add_wait(o, w)
        for wb in writes:
            w = wb.last_w
            if w is not None and (w.is_dma or dma or w.eng != eng):
                self._add_wait(o, w)
            for rd in wb.readers:
                if rd.is_dma or dma or rd.eng != eng:
                    self._add_wait(o, rd)
        for r in reads:
            r.readers.append(o)
        for wb in writes:
            wb.last_w = o
            wb.readers = []
        self.ops.append(o)
        self.last_op[eng] = o
        return o

    def barrier(self):
        o = Op(len(self.ops), "sp", lambda e: e.nop(), False)
        for E in ENGS:
            lo = self.last_op[E]
            if lo is not None and not lo.is_dma and E != "sp":
                self._add_wait(o, lo)
        for did in sorted(self.unconsumed_dma):
            self._add_wait(o, self.ops[did])
        self.ops.append(o)
        self.last_op["sp"] = o
        for E in ENGS:
            if E == "sp":
                continue
            o2 = Op(len(self.ops), E, lambda e: e.nop(), False)
            self._add_wait(o2, o)
            self.ops.append(o2)
            self.last_op[E] = o2
        return o

    def dma(self, out, in_, reads=(), writes=(), q="sp"):
        return self.op(q, lambda e: e.dma_start(out=out, in_=in_), reads=reads, writes=writes, dma=True)

    def mm(self, out, lhsT, rhs, start, stop, reads, writes):
        return self.op("pe", lambda e: e.matmul(out, lhsT=lhsT, rhs=rhs, start=start, stop=stop), reads=reads, writes=writes)

    def tr(self, out, in_, ident, reads, writes):
        return self.op("pe", lambda e: e.transpose(out=out, in_=in_, identity=ident), reads=reads, writes=writes)

    def tt(self, eng, out, in0, in1, op, reads, writes):
        return self.op(eng, lambda e: e.tensor_tensor(out=out, in0=in0, in1=in1, op=op), reads=reads, writes=writes)

    def ts(self, eng, out, in0, s1, s2, op0, op1, reads, writes):
        if op1 is None:
            return self.op(eng, lambda e: e.tensor_scalar(out=out, in0=in0, scalar1=s1, scalar2=None, op0=op0), reads=reads, writes=writes)
        return self.op(eng, lambda e: e.tensor_scalar(out=out, in0=in0, scalar1=s1, scalar2=s2, op0=op0, op1=op1), reads=reads, writes=writes)

    def stt(self, out, in0, scalar, in1, op0, op1, reads, writes):
        return self.op("dve", lambda e: e.scalar_tensor_tensor(out=out, in0=in0, scalar=scalar, in1=in1, op0=op0, op1=op1), reads=reads, writes=writes)

    def act(self, out, in_, func, reads, writes, bias=None, scale=None):
        kw = {}
        if bias is not None:
            kw["bias"] = bias
        if scale is not None:
            kw["scale"] = scale
        return self.op("act", lambda e: e.activation(out=out, in_=in_, func=func, **kw), reads=reads, writes=writes)

    def cp(self, eng, out, in_, reads, writes):
        if eng == "act":
            return self.op("act", lambda e: e.copy(out=out, in_=in_), reads=reads, writes=writes)
        return self.op(eng, lambda e: e.tensor_copy(out=out, in_=in_), reads=reads, writes=writes)

    def memset(self, eng, ap, val, writes):
        return self.op(eng, lambda e: e.memset(ap, val), writes=writes)

    def emit(self):
        nc = self.nc
        self.barrier()
        cnt = {e: 0 for e in ENGS}
        for o in self.ops:
            if not o.is_dma and o.signals:
                cnt[o.eng] += 1
                o.val = cnt[o.eng]
        st = self.stack
        esem = {e: st.enter_context(nc.semaphore(f"s_{e}")) for e in ENGS}
        dsem = {}
        for e in ENGS:
            if any(o.is_dma and o.eng == e for o in self.ops):
                for k in range(self.N_DMA_SEMS):
                    dsem[(e, k)] = st.enter_context(nc.semaphore(f"d_{e}_{k}"))
        by_eng = {e: [o for o in self.ops if o.eng == e] for e in ENGS}

        def run(engine_obj, e):
            for o in by_eng[e]:
                for d in o.waits:
                    if d.is_dma:
                        engine_obj.wait_ge(dsem[(d.eng, d.sem)], d.val)
                    else:
                        engine_obj.wait_ge(esem[d.eng], d.val)
                ins = o.fn(engine_obj)
                if o.is_dma:
                    ins.then_inc(dsem[(e, o.sem)], 16)
                elif o.signals:
                    ins.then_inc(esem[e], 1)

        with nc.Block() as block:
            @block.tensor
            def _(eng):
                run(eng, "pe")

            @block.vector
            def _(eng):
                run(eng, "dve")

            @block.scalar
            def _(eng):
                run(eng, "act")

            @block.gpsimd
            def _(eng):
                run(eng, "pool")

            @block.sync
            def _(eng):
                run(eng, "sp")
        self.stack.close()
        return nc

C_ID, C_PERM, C_BONES, C_MSU, C_MSL, C_IU64, C_SU64, C_AMASK, C_RESET, C_INVF, C_SGN, C_HIND, C_N = (
    0, 128, 256, 384, 512, 640, 704, 768, 1280, 2304, 2305, 2306, 2312)


def build_consts():
    c = np.zeros((128, C_N), np.float32)
    c[:, C_ID:C_ID + 128] = np.eye(128, dtype=np.float32)
    for m in range(128):
        pm = m + 32 if (m % 64) < 32 else m - 32
        c[pm, C_PERM + m] = 1.0
    c[0:64, C_BONES:C_BONES + 64] = 1.0
    c[64:128, C_BONES + 64:C_BONES + 128] = 1.0
    s = np.arange(64)
    su = (s[:, None] < s[None, :]).astype(np.float32)
    sl = (s[:, None] > s[None, :]).astype(np.float32)
    iu = (s[:, None] <= s[None, :]).astype(np.float32)
    for g in range(2):
        c[64 * g:64 * g + 64, C_MSU + 64 * g:C_MSU + 64 * g + 64] = su
        c[64 * g:64 * g + 64, C_MSL + 64 * g:C_MSL + 64 * g + 64] = sl
        c[64 * g:64 * g + 64, C_IU64:C_IU64 + 64] = iu
        c[64 * g:64 * g + 64, C_SU64:C_SU64 + 64] = su
    kq = np.arange(128)
    prev = (kq[:, None] > kq[None, :]).astype(np.float32)
    cur = (kq[:, None] <= kq[None, :]).astype(np.float32)
    am = np.zeros((128, 2, 2, 128), np.float32)
    am[:, 0, :, :] = prev[:, None, :]
    am[:, 1, :, :] = cur[:, None, :]
    c[:, C_AMASK:C_AMASK + 512] = am.reshape(128, 512)
    rm = np.ones((128, 1024), np.float32)
    rm[:, ::64] = 0.0
    c[:, C_RESET:C_RESET + 1024] = rm
    inv = (10000.0 ** (-np.arange(0, 64, 2, dtype=np.float32) / 64)).astype(np.float32)
    for p in range(128):
        c[p, C_INVF] = inv[p % 32]
        c[p, C_SGN] = -1.0 if (p % 64) < 32 else 1.0
    c[0:64, C_HIND] = 1.0
    c[64:128, C_HIND + 1] = 1.0
    return c


def col_layout(v, n=None):
    v = np.asarray(v, np.float32).reshape(-1)
    n = n or -(-v.size // 128)
    out = np.zeros(n * 128, np.float32)
    out[:v.size] = v
    return np.ascontiguousarray(out.reshape(n, 128).T)


LP_MU, LP_W0, LP_A0, LP_KK, LP_KA, LP_RK, LP_V0, LP_N = 0, 27, 35, 43, 51, 59, 67, 80


def build_lp(inp, l):
    lp = np.zeros((128, LP_N), np.float32)
    lp[:, LP_MU:LP_MU + 27] = col_layout(inp["rwkv_mu"][l], 27)
    lp[:, LP_W0:LP_W0 + 8] = col_layout(inp["rwkv_w0"][l])
    lp[:, LP_A0:LP_A0 + 8] = col_layout(inp["rwkv_a0"][l])
    lp[:, LP_KK:LP_KK + 8] = col_layout(inp["rwkv_k_k"][l])
    lp[:, LP_KA:LP_KA + 8] = col_layout(inp["rwkv_k_a"][l])
    lp[:, LP_RK:LP_RK + 8] = col_layout(inp["rwkv_r_k"][l].reshape(-1))
    if l > 0:
        lp[:, LP_V0:LP_V0 + 8] = col_layout(inp["rwkv_v0"][l - 1])
    return lp


class Ctx:
    pass


def setup_consts(P, consts_ap, cflags_ap):
    K = Ctx()
    K.c = P.alloc([128, C_N], F32, "consts")
    P.dma(K.c[:], consts_ap, writes=[K.c])
    K.fl = P.alloc([128, 16], F32, "cflags")
    P.dma(K.fl[:], cflags_ap, writes=[K.fl])
    K.ident = K.c[:, C_ID:C_ID + 128]
    K.identb = P.alloc([128, 128], BF16, "identb")
    P.cp("dve", K.identb[:], K.c[:, C_ID:C_ID + 128], [K.c], [K.identb])
    K.permb = P.alloc([128, 128], BF16, "permb")
    P.cp("dve", K.permb[:], K.c[:, C_PERM:C_PERM + 128], [K.c], [K.permb])
    K.bonesb = P.alloc([128, 128], BF16, "bonesb")
    P.cp("dve", K.bonesb[:], K.c[:, C_BONES:C_BONES + 128], [K.c], [K.bonesb])
    K.hindb = P.alloc([128, 2], BF16, "hindb")
    P.cp("dve", K.hindb[:], K.c[:, C_HIND:C_HIND + 2], [K.c], [K.hindb])
    K.onesb = P.alloc([128, 128], BF16, "onesb")
    P.memset("dve", K.onesb[:], 1.0, [K.onesb])
    K.amask = P.alloc([128, 512], BF16, "amask")
    P.cp("dve", K.amask[:], K.c[:, C_AMASK:C_AMASK + 512], [K.c], [K.amask])
    K.amask0 = P.alloc([128, 512], BF16, "amask0")
    P.cp("dve", K.amask0[:, 256:512], K.c[:, C_AMASK + 256:C_AMASK + 512], [K.c], [K.amask0])
    P.ts("dve", K.amask0[:, 0:256], K.c[:, C_AMASK:C_AMASK + 256], K.fl[:, 0:1], None, ALU.mult, None, [K.c, K.fl], [K.amask0])
    return K


def build_mod_program():
    P = Prog()
    c_col = P.dram_in("c_col", [128, 16])
    wa = P.dram_in("wa", [2048, 3072])
    ba = P.dram_in("ba_col", [128, 24])
    consts = P.dram_in("consts", [128, C_N])
    out = P.dram_out("mod_out", [24, 128])
    cc = P.alloc([128, 16], F32, "cc")
    P.dma(cc[:], c_col, writes=[cc])
    bc = P.alloc([128, 24], F32, "bc")
    P.dma(bc[:], ba, writes=[bc])
    ident = P.alloc([128, 128], F32, "ident")
    P.dma(ident[:], consts[:, C_ID:C_ID + 128], writes=[ident])
    cond = P.alloc([128, 16], F32, "cond")
    P.act(cond[:], cc[:], AF.Silu, [cc], [cond])
    wbuf = [P.alloc([128, 16, 512], F32, f"wb{i}") for i in range(2)]
    wav = wa.rearrange("(kc p) n -> p kc n", p=128)
    res = P.alloc([128, 24], F32, "res")
    bk = P.bank()
    for blk in range(6):
        wb = wbuf[blk % 2]
        P.dma(wb[:], wav[:, :, blk * 512:(blk + 1) * 512], writes=[wb], q=("sp" if blk % 2 == 0 else "act"))
        for cix in range(4):
            col = blk * 4 + cix
            for kc in range(KC):
                P.mm(bk[:, col:col + 1], wb[:, kc, cix * 128:(cix + 1) * 128], cond[:, kc:kc + 1], kc == 0, kc == KC - 1, [wb, cond], [bk])
    P.tt("dve", res[:], bk[:, 0:24], bc[:], ALU.add, [bk, bc], [res])
    bk2 = P.bank()
    P.tr(bk2[0:24, 0:128], res[:], ident[:], [res, ident], [bk2])
    rt = P.alloc([24, 128], F32, "rt")
    P.cp("dve", rt[:], bk2[0:24, 0:128], [bk2], [rt])
    P.dma(out, rt[:], reads=[rt])
    return P.emit()

MAGIC = 12582912.0
PI_SAFE = 3.1415925


def load_mod_cols(P, K, modg, l):
    modr = P.alloc([128, 128], F32, "modr")
    P.dma(modr[0:96, :], modg[l * 96:(l + 1) * 96, :], writes=[modr])
    bk = P.bank()
    P.tr(bk[:, 0:96], modr[0:96, :], K.ident[0:96, 0:96], [modr, K.c], [bk])
    modc = P.alloc([128, 96], F32, "modc")
    P.cp("dve", modc[:], bk[:, 0:96], [bk], [modc])
    return modc


def rope_tables(P, K, pos):
    sinS = P.alloc([128, 1152], F32, "sinS")
    cos2 = P.alloc([128, 1152], F32, "cos2")
    mk = P.mark()
    posi = P.alloc([128, 1152], I32, "posi")
    P.dma(posi[:], pos.broadcast_to([128, 1152]), writes=[posi])
    ang = P.alloc([128, 1152], F32, "ang")
    P.cp("dve", ang[:], posi[:], [posi], [ang])
    P.ts("dve", ang[:], ang[:], K.c[:, C_INVF:C_INVF + 1], None, ALU.mult, None, [ang, K.c], [ang])
    n = P.alloc([128, 1152], F32, "ropen")
    P.ts("dve", n[:], ang[:], float(1.0 / (2 * np.pi)), MAGIC, ALU.mult, ALU.add, [ang], [n])
    P.ts("dve", n[:], n[:], -MAGIC, None, ALU.add, None, [n], [n])
    r = P.alloc([128, 1152], F32, "roper")
    P.stt(r[:], n[:], -TWO_PI_HI, ang[:], ALU.mult, ALU.add, [n, ang], [r])
    P.stt(r[:], n[:], -TWO_PI_LO, r[:], ALU.mult, ALU.add, [n, r], [r])
    P.ts("dve", r[:], r[:], -PI_SAFE, PI_SAFE, ALU.max, ALU.min, [r], [r])
    P.act(sinS[:], r[:], AF.Sin, [r], [sinS])
    P.ts("dve", sinS[:], sinS[:], K.c[:, C_SGN:C_SGN + 1], None, ALU.mult, None, [sinS, K.c], [sinS])
    P.ts("dve", n[:], r[:], -1.0, None, ALU.mult, None, [r], [n])
    P.tt("dve", n[:], n[:], r[:], ALU.max, [n, r], [n])
    P.ts("dve", n[:], n[:], -1.0, float(np.pi / 2), ALU.mult, ALU.add, [n], [n])
    P.act(cos2[:], n[:], AF.Sin, [n], [cos2])
    P.release(mk)
    return cos2, sinS


def make_uT(P, K, x_srcs, sc1, sh, uT, xt):
    for tt, src in enumerate(x_srcs):
        xb = xt[tt % 2]
        P.dma(xb[:], src, writes=[xb], q=("sp" if tt % 2 == 0 else "act"))
        for f4 in range(4):
            bk = P.bank()
            for q in range(4):
                fc = f4 * 4 + q
                P.tr(bk[:, q * 128:(q + 1) * 128], xb[:, fc * 128:(fc + 1) * 128], K.ident, [xb, K.c], [bk])
            for q in range(4):
                fc = f4 * 4 + q
                o = uT[:, fc, tt * 128:(tt + 1) * 128]
                i_ = bk[:, q * 128:(q + 1) * 128]
                if q % 2 == 0:
                    P.ts("dve", o, i_, sc1[:, fc:fc + 1], sh[:, fc:fc + 1], ALU.mult, ALU.add, [bk, sc1, sh], [uT])
                else:
                    P.act(o, i_, AF.Identity, [bk, sc1, sh], [uT], bias=sh[:, fc:fc + 1], scale=sc1[:, fc:fc + 1])


class WStream:
    def __init__(self, P, nkc, maxcols, nbuf=2, name="wb"):
        self.P = P
        self.bufs = [P.alloc([128, nkc, maxcols], BF16, f"{name}{i}") for i in range(nbuf)]
        self.i = 0

    def load(self, src_view, col0, ncols, kc0=0, nkc=None):
        wb = self.bufs[self.i % len(self.bufs)]
        self.i += 1
        nkc = nkc or src_view.shape[1]
        self.P.dma(wb[:, 0:nkc, 0:ncols], src_view[:, kc0:kc0 + nkc, col0:col0 + ncols], writes=[wb], q="pool")
        return wb


def proj_fm(P, out_ap, bank, wb, c0, m, uT, tok0, n, nkc=KC):
    for kc in range(nkc):
        P.mm(out_ap, wb[:, kc, c0:c0 + m], uT[:, kc, tok0:tok0 + n], kc == 0, kc == nkc - 1, [wb, uT], [bank])


ARENA_WORDS = 48640


def build_p1(l, dbg=False):
    P = Prog(ARENA_WORDS)
    x_in = P.dram_in("x_in", [1024, 2048])
    x_halo = P.dram_in("x_halo", [128, 2048])
    pos = P.dram_in("pos", [1, 1152], I32)
    modg = P.dram_in("modg", [192, 128])
    consts = P.dram_in("consts", [128, C_N])
    cflags = P.dram_in("cflags", [128, 16])
    lp_in = P.dram_in("lp", [128, LP_N])
    w_in = P.dram_in("w_in", [2048, NIN])
    sinks = P.dram_in("sinks", [1, 16])
    w2_in = P.dram_in("w2", [64, 1024])
    a2_in = P.dram_in("a2", [64, 1024])
    if l > 0:
        v1_in = P.dram_in("v1", [1024, 32])
        v2_in = P.dram_in("v2", [32, 1024])
        vf_in = P.dram_in("vfirst", [128, 8 * 1024 // 2])
    else:
        vf_out = P.dram_out("vfirst_o", [128, 8 * 1024 // 2])
    yattn_o = P.dram_out("yattn", [128, 8, 512])
    oprime_o = P.dram_out("oprime", [128, 8, 512])
    ompt_o = P.dram_out("ompt", [128, 8, 512])
    bonus_o = P.dram_out("bonus", [128, 8, 512])
    sgd_o = P.dram_out("sgd", [128, 2 * 512])
    zq_o = P.dram_out("zq", [128, 8 * 128])
    dbg_outs = {}

    K = setup_consts(P, consts, cflags)
    lp = P.alloc([128, LP_N], F32, "lp")
    P.dma(lp[:], lp_in, writes=[lp])
    modc = load_mod_cols(P, K, modg, l)
    sc1 = P.alloc([128, 16], F32, "sc1")
    P.ts("dve", sc1[:], modc[:, 16:32], 1.0, None, ALU.add, None, [modc], [sc1])
    sh = Tl(modc[:, 0:16], modc.b)
    XR = P.alloc([128, 8, 1024], BF16, "XR")
    XK = P.alloc([128, 8, 1024], BF16, "XK")
    XV = P.alloc([128, 8, 1024], BF16, "XV")
    LI = P.alloc([128, 1024], BF16, "LI")
    SG = P.alloc([128, 2, 1024], BF16, "SG")
    P.memset("pool", SG[:, 1, :], 0.0, [SG])
    mkA = P.mark()
    P.hi_mode = True

    uT = P.alloc([128, KC, 1152], BF16, "uT")
    m1 = P.mark()
    xt = [P.alloc([128, 2048], F32, f"xt{i}") for i in range(2)]
    srcs = [x_halo] + [x_in[t * 128:(t + 1) * 128, :] for t in range(8)]
    make_uT(P, K, srcs, sc1, sh, uT, xt)
    P.release(m1)
    W = WStream(P, KC, 512)
    m2 = P.mark()
    cos2, sinS = rope_tables(P, K, pos)
    esink = P.alloc([128, 16, 128], BF16, "esink")
    P.memset("pool", esink[:], 0.0, [esink])
    sk = P.alloc([128, 16], F32, "sk")
    P.dma(sk[0:1, :], sinks, writes=[sk])
    P.act(esink[0:1, :, :], sk[0:1, :].unsqueeze(2).broadcast_to([1, 16, 128]), AF.Exp, [sk], [esink])
    sinkL = P.alloc([128, 64], BF16, "sinkL")
    P.memset("pool", sinkL[:], 0.0, [sinkL])
    P.memset("pool", sinkL[0:1, :], 1.0, [sinkL])
    w_view = w_in.rearrange("(kc p) n -> p kc n", p=128)
    qT = P.alloc([128, 8, 1024], BF16, "qT")
    kT2 = P.alloc([128, 4, 1152], BF16, "kT2")
    v_tm = P.alloc([128, 9, 256], BF16, "v_tm")
    m3 = P.mark()
    qraw = [P.alloc([128, 512], BF16, f"qraw{i}") for i in range(2)]
    rt1 = [P.alloc([128, 512], F32, f"rt1_{i}") for i in range(2)]
    rt2 = [P.alloc([128, 512], F32, f"rt2_{i}") for i in range(2)]
    cnt = [0]

    def rope_from_bank(bk, n, tok0, out_ap, out_tile):
        i = cnt[0] % 2
        cnt[0] += 1
        P.cp("act", qraw[i][:, 0:n], bk[:, 0:n], [bk], [qraw[i]])
        bk2 = P.bank()
        P.mm(bk2[:, 0:n], K.permb[:], qraw[i][:, 0:n], True, True, [K.permb, qraw[i]], [bk2])
        P.tt("pool", rt1[i][:, 0:n], qraw[i][:, 0:n], cos2[:, tok0:tok0 + n], ALU.mult, [qraw[i], cos2], [rt1[i]])
        P.tt("dve", rt2[i][:, 0:n], bk2[:, 0:n], sinS[:, tok0:tok0 + n], ALU.mult, [bk2, sinS], [rt2[i]])
        P.tt("pool", out_ap, rt1[i][:, 0:n], rt2[i][:, 0:n], ALU.add, [rt1[i], rt2[i]], [out_tile])

    for blk in range(2):
        wb = W.load(w_view, blk * 512, 512)
        for cix in range(4):
            qc = blk * 4 + cix
            for half in range(2):
                bk = P.bank()
                proj_fm(P, bk[:, 0:512], bk, wb, cix * 128, 128, uT, 128 + half * 512, 512)
                rope_from_bank(bk, 512, 128 + half * 512, qT[:, qc, half * 512:(half + 1) * 512], qT)
    wb = W.load(w_view, 1024, 512)
    for g in range(4):
        for (tok0, n) in ((0, 128), (128, 512), (640, 512)):
            bk = P.bank()
            for dup in range(2):
                proj_fm(P, bk[64 * dup:64 * dup + 64, 0:n], bk, wb, g * 64, 64, uT, tok0, n)
            rope_from_bank(bk, n, tok0, kT2[:, g, tok0:tok0 + n], kT2)
    for tt in range(9):
        bk = P.bank()
        for kc in range(KC):
            P.mm(bk[:, 0:256], uT[:, kc, tt * 128:(tt + 1) * 128], wb[:, kc, 256:512], kc == 0, kc == KC - 1, [uT, wb], [bk])
        P.cp("act", v_tm[:, tt, :], bk[:, 0:256], [bk], [v_tm])

    P.release(m3)
    Ee = [P.alloc([128, 512], BF16, f"Ee{i}") for i in range(2)]
    Eo = [P.alloc([128, 512], BF16, f"Eo{i}") for i in range(2)]
    rec = [P.alloc([128, 256], F32, f"rec{i}") for i in range(2)]
    yblk = [P.alloc([128, 8, 128], BF16, f"yblk{i}") for i in range(2)]
    it = 0
    for i in range(8):
        yb = yblk[i % 2]
        for g in range(4):
            E = (Ee[it % 2], Eo[it % 2])
            rc = rec[it % 2]
            it += 1
            bS = (P.bank(), P.bank())
            for hl in range(4):
                hp, he = hl % 2, hl // 2
                ch = 2 * g + he
                for kb in range(2):
                    P.mm(bS[hp][:, (kb * 2 + he) * 128:(kb * 2 + he + 1) * 128],
                         kT2[64 * hp:64 * hp + 64, g, (i + kb) * 128:(i + kb + 1) * 128],
                         qT[64 * hp:64 * hp + 64, ch, i * 128:(i + 1) * 128], True, True, [kT2, qT], [bS[hp]])
            msk = K.amask0 if i == 0 else K.amask
            for hp in range(2):
                P.act(E[hp][:], bS[hp][:, 0:512], AF.Exp, [bS[hp]], [E[hp]], scale=0.125)
                P.tt("pool", E[hp][:], E[hp][:], msk[:], ALU.mult, [E[hp], msk], [E[hp]])
            bO = P.bank()
            for hl in range(4):
                hp, he = hl % 2, hl // 2
                h = 4 * g + hl
                for kb in range(2):
                    P.mm(bO[64 * hp:64 * hp + 64, he * 128:(he + 1) * 128], v_tm[:, i + kb, g * 64:(g + 1) * 64],
                         E[hp][:, (kb * 2 + he) * 128:(kb * 2 + he + 1) * 128], kb == 0, kb == 1, [v_tm, E[hp]], [bO])
                for kb in range(2):
                    P.mm(bO[64 * hp:64 * hp + 64, 256 + he * 128:256 + (he + 1) * 128], K.onesb[:, 0:64],
                         E[hp][:, (kb * 2 + he) * 128:(kb * 2 + he + 1) * 128], kb == 0, False, [K.onesb, E[hp]], [bO])
                P.mm(bO[64 * hp:64 * hp + 64, 256 + he * 128:256 + (he + 1) * 128], sinkL[:, 0:64],
                     esink[:, h, :], False, True, [sinkL, esink], [bO])
            P.op("dve", lambda e, rc=rc, bO=bO: e.reciprocal(out=rc[:], in_=bO[:, 256:512]), reads=[bO.b], writes=[rc.b])
            P.tt("dve", yb[:, 2 * g:2 * g + 2, :], bO[:, 0:256].rearrange("p (a b) -> p a b", b=128),
                 rc[:].rearrange("p (a b) -> p a b", b=128), ALU.mult, [bO, rc], [yb])
        P.dma(yattn_o[:, i, :].bitcast(BF16), yb[:].rearrange("p a b -> p (a b)"), reads=[yb])

    P.release(m2)
    raw = [P.alloc([128, 1025], F32, f"raw{i}") for i in range(2)]
    dsh = [P.alloc([128, 1024], F32, f"dsh{i}") for i in range(2)]
    xsf = P.alloc([128, 1024], F32, "xsf")
    chunk_specs = []
    for sec in range(3):
        for blk in range(2):
            chunk_specs.append((RB + sec * 1024 + blk * 512, 512, [(sec * 8 + blk * 4 + q, q * 128, 128) for q in range(4)]))
    chunk_specs.append((RB + 3072, 288, [(24, 0, 128), (25, 128, 128), (26, 256, 32)]))
    ci = 0
    for (col0, ncols, chunks) in chunk_specs:
        wb = W.load(w_view, col0, ncols)
        for (cc, c0, m) in chunks:
            rw = raw[ci % 2]
            ds = dsh[ci % 2]
            ci += 1
            bh = P.bank()
            proj_fm(P, bh[0:m, 0:1], bh, wb, c0, m, uT, 127, 1)
            bA = P.bank()
            proj_fm(P, bA[0:m, 0:512], bA, wb, c0, m, uT, 128, 512)
            bB = P.bank()
            proj_fm(P, bB[0:m, 0:512], bB, wb, c0, m, uT, 640, 512)
            P.ts("dve", rw[0:m, 0:1], bh[0:m, 0:1], K.fl[0:m, 0:1], None, ALU.mult, None, [bh, K.fl], [rw])
            P.cp("act", rw[0:m, 1:513], bA[0:m, 0:512], [bA], [rw])
            P.cp("dve", rw[0:m, 513:1025], bB[0:m, 0:512], [bB], [rw])
            P.tt("pool", ds[0:m, :], rw[0:m, 0:1024], rw[0:m, 1:1025], ALU.subtract, [rw], [ds])
            mu = lp[0:m, LP_MU + cc:LP_MU + cc + 1]
            if cc < 24:
                dst = (XR, XK, XV)[cc // 8]
                P.stt(dst[:, cc % 8, :], ds[:, :], mu, rw[:, 1:1025], ALU.mult, ALU.add, [ds, lp, rw], [dst])
            else:
                P.stt(xsf[0:m, :], ds[0:m, :], mu, rw[0:m, 1:1025], ALU.mult, ALU.add, [ds, lp, rw], [xsf])
                if cc == 24:
                    P.act(LI[0:64, :], xsf[0:64, :], AF.Tanh, [xsf], [LI])
                    P.cp("act", LI[64:128, :], xsf[64:128, :], [xsf], [LI])
                elif cc == 25:
                    P.act(SG[:, 0, :], xsf[:, :], AF.Sigmoid, [xsf], [SG])
                else:
                    P.act(SG[0:32, 1, :], xsf[0:32, :], AF.Sigmoid, [xsf], [SG])
    P.dma(sgd_o.bitcast(BF16), SG[:].rearrange("p a b -> p (a b)"), reads=[SG])
    if l == 0:
        P.dma(vf_out.bitcast(BF16), XV[:].rearrange("p a b -> p (a b)"), reads=[XV])
    if dbg:
        for nm, t_, n_ in (("d_qT", qT, 8 * 1024), ("d_kT2", kT2, 4 * 1152), ("d_XR", XR, 8192), ("d_XK", XK, 8192), ("d_XV", XV, 8192), ("d_LI", LI, 1024)):
            o = P.dram_out(nm, [128, n_ // 2])
            ap = t_[:] if len(t_.ap.shape) == 2 else t_[:].rearrange("p a b -> p (a b)")
            P.dma(o.bitcast(BF16), ap, reads=[t_])
    P.release(mkA)
    P.hi_mode = False
    return P, dict(K=K, lp=lp, XR=XR, XK=XK, XV=XV, LI=LI, w2_in=w2_in, a2_in=a2_in, l=l,
                   oprime_o=oprime_o, ompt_o=ompt_o, bonus_o=bonus_o, zq_o=zq_o,
                   v1_in=(v1_in if l > 0 else None), v2_in=(v2_in if l > 0 else None), vf_in=(vf_in if l > 0 else None))

def rwkv_phase_b(P, S):
    K, lp, XR, XK, XV, LI, l = S["K"], S["lp"], S["XR"], S["XK"], S["XV"], S["LI"], S["l"]
    c = K.c
    f32t = lambda nm: P.alloc([128, 8, 128], F32, nm)
    bft = lambda nm: P.alloc([128, 8, 128], BF16, nm)
    bf64 = lambda nm: P.alloc([128, 8, 64], BF16, nm)
    w2a2 = P.alloc([128, 1024], BF16, "w2a2")
    P.dma(w2a2[0:64, :], S["w2_in"], writes=[w2a2], q="pool")
    P.dma(w2a2[64:128, :], S["a2_in"], writes=[w2a2], q="pool")
    if l > 0:
        v1b = P.alloc([128, 8, 32], BF16, "v1b")
        P.dma(v1b[:], S["v1_in"].rearrange("(j p) n -> p j n", p=128), writes=[v1b], q="pool")
        v2b = P.alloc([128, 1024], BF16, "v2b")
        P.dma(v2b[0:32, :], S["v2_in"], writes=[v2b], q="pool")
        VF = P.alloc([128, 8, 128], BF16, "VF")
        t1b = P.alloc([128, 128], BF16, "t1b")
    omka = P.alloc([128, 8], F32, "omka")
    P.ts("dve", omka[:], lp[:, LP_KA:LP_KA + 8], -1.0, 1.0, ALU.mult, ALU.add, [lp], [omka])

    def bc8(col0):
        return lp[:, col0:col0 + 8].unsqueeze(2).broadcast_to([128, 8, 128])

    T0, T1, T2, T3, T4, T5 = [f32t(f"T{i}") for i in range(6)]
    At, Bt, Kt, Rt, Kh, Bh, Vf, rkb, kk2b = [bft(n) for n in ("At", "Bt", "Kt", "Rt", "Kh", "Bh", "Vf", "rkb", "kk2b")]
    A_tm, V_tm, Bh_tm, Kh_tm, R_tm = [bft(n) for n in ("A_tm", "V_tm", "Bh_tm", "Kh_tm", "R_tm")]
    Yc = [f32t("Yc0"), f32t("Yc1")]
    Xc = [f32t("Xc0"), f32t("Xc1")]
    Pt = f32t("Pt")
    TTb = bft("TTb")
    MrbT_s, LakT_s, MrkT_s = bf64("MrbT_s"), bf64("LakT_s"), bf64("MrkT_s")
    AY = bft("AY")
    WUs = bft("WUs")
    LRs = [bf64("LRsX"), bf64("LRsY")]
    OmTs = [bf64("OmTsX"), bf64("OmTsY")]
    Ds = [bf64("DsX"), bf64("DsY")]
    Ols = [P.alloc([128, 8, 64], F32, "OlsX"), P.alloc([128, 8, 64], F32, "OlsY")]
    Z = f32t("Z")
    Zb = bft("Zb")
    Ztmp = f32t("Ztmp")
    gate = Ztmp
    gamC = P.alloc([128, 8, 2], F32, "gamC")
    OmpTt = [bft("OmpTt0"), bft("OmpTt1")]
    Opt = [P.alloc([128, 16, 64], BF16, "Opt0"), P.alloc([128, 16, 64], BF16, "Opt1")]
    bon = [P.alloc([128, 16, 64], BF16, "bon0"), P.alloc([128, 16, 64], BF16, "bon1")]
    rks = P.alloc([128, 16], F32, "rks")
    P.memset("pool", Z[:], 0.0, [Z])
    for hl in range(2):
        P.cp("pool", Z[64 * hl:64 * hl + 64, :, 64:128],
             c[64 * hl:64 * hl + 64, C_ID + 64 * hl:C_ID + 64 * hl + 64].unsqueeze(1).broadcast_to([64, 8, 64]), [c], [Z])
    P.nrot = 6
    P.bank_rr = 0
    bOd = [P.banks[6], P.banks[7]]
    v16 = lambda t: t[:].rearrange("p j (c t) -> p (j c) t", t=64)

    for i in range(8):
        tk = slice(i * 128, (i + 1) * 128)
        r_in, k_in, v_in = XR[:, :, tk], XK[:, :, tk], XV[:, :, tk]
        bw = (P.bank(), P.bank())
        ba = (P.bank(), P.bank())
        for j in range(8):
            P.mm(bw[j // 4][:, (j % 4) * 128:(j % 4 + 1) * 128], w2a2[0:64, j * 128:(j + 1) * 128], LI[0:64, tk], True, True, [w2a2, LI], [bw[j // 4]])
        for j in range(8):
            P.mm(ba[j // 4][:, (j % 4) * 128:(j % 4 + 1) * 128], w2a2[64:128, j * 128:(j + 1) * 128], LI[64:128, tk], True, True, [w2a2, LI], [ba[j // 4]])
        for j in range(8):
            P.act(T0[:, j, :], bw[j // 4][:, (j % 4) * 128:(j % 4 + 1) * 128], AF.Sigmoid, [bw[j // 4], lp], [T0], bias=lp[:, LP_W0 + j:LP_W0 + j + 1])
            P.act(T1[:, j, :], ba[j // 4][:, (j % 4) * 128:(j % 4 + 1) * 128], AF.Sigmoid, [ba[j // 4], lp], [T1], bias=lp[:, LP_A0 + j:LP_A0 + j + 1])
        P.ts("dve", T0[:], T0[:], NEG_EXPHALF, None, ALU.mult, None, [T0], [T0])
        flat = lambda t: t[:].rearrange("p a b -> p (a b)")
        P.op("dve", lambda e: e.tensor_tensor_scan(out=flat(T2), data0=c[:, C_RESET:C_RESET + 1024], data1=flat(T0), initial=0.0,
                                                   op0=ALU.mult, op1=ALU.add), reads=[c.b if isinstance(c, Tl) else c, T0.b], writes=[T2.b])
        P.tt("pool", T3[:], T2[:], T0[:], ALU.subtract, [T2, T0], [T3])
        P.act(T3[:], T3[:], AF.Exp, [T3], [T3])
        P.act(T4[:], T2[:], AF.Exp, [T2], [T4])
        P.act(T5[:], T2[:], AF.Exp, [T2], [T5], scale=-1.0)
        P.cp("dve", gamC[:].rearrange("p j c -> p (j c)").unsqueeze(2), v16(T4)[:, :, 63:64], [T4], [gamC])
        P.tt("pool", T0[:], k_in, bc8(LP_KK), ALU.mult, [XK, lp], [T0])
        P.tt("pool", kk2b[:], T0[:], T0[:], ALU.mult, [T0], [kk2b])
        bss = (P.bank(), P.bank())
        for j in range(8):
            P.mm(bss[j // 4][:, (j % 4) * 128:(j % 4 + 1) * 128], K.bonesb[:], kk2b[:, j, :], True, True, [K.bonesb, kk2b], [bss[j // 4]])
        for hb in range(2):
            P.ts("dve", T2[:, hb * 4:(hb + 1) * 4, :], bss[hb][:, 0:512].rearrange("p (a b) -> p a b", b=128), 1e-24, None, ALU.max, None, [bss[hb]], [T2])
        P.act(T2[:], T2[:], AF.Sqrt, [T2], [T2])
        P.op("dve", lambda e: e.reciprocal(out=T2[:], in_=T2[:]), reads=[T2.b], writes=[T2.b])
        P.tt("dve", T0[:], T0[:], T2[:], ALU.mult, [T0, T2], [T0])
        P.tt("pool", T2[:], T1[:], bc8(LP_KA), ALU.mult, [T1, lp], [T2])
        P.tt("pool", T2[:], T2[:], omka[:].unsqueeze(2).broadcast_to([128, 8, 128]), ALU.add, [T2, omka], [T2])
        P.tt("pool", T2[:], T2[:], k_in, ALU.mult, [T2, XK], [T2])
        P.tt("dve", T1[:], T1[:], T0[:], ALU.mult, [T1, T0], [T1])
        P.stt(At[:], T0[:], -1.0, T3[:], ALU.mult, ALU.mult, [T0, T3], [At])
        P.tt("pool", T3[:], T2[:], T5[:], ALU.mult, [T2, T5], [T3])
        P.tt("dve", T0[:], T1[:], T5[:], ALU.mult, [T1, T5], [T0])
        P.cp("act", Kt[:], T3[:], [T3], [Kt])
        P.cp("act", Bt[:], T0[:], [T0], [Bt])
        gam = v16(T4)[:, :, 63:64].broadcast_to([128, 16, 64])
        P.tt("pool", v16(Kh), v16(T3), gam, ALU.mult, [T3, T4], [Kh])
        P.tt("dve", v16(Bh), v16(T0), gam, ALU.mult, [T0, T4], [Bh])
        P.tt("pool", Rt[:], r_in, T4[:], ALU.mult, [XR, T4], [Rt])
        P.tt("pool", T5[:], r_in, bc8(LP_RK), ALU.mult, [XR, lp], [T5])
        P.tt("dve", rkb[:], T5[:], T2[:], ALU.mult, [T5, T2], [rkb])
        if l == 0:
            P.cp("pool", Vf[:], v_in, [XV], [Vf])
        else:
            P.dma(VF[:], S["vf_in"].bitcast(BF16).rearrange("p (j t) -> p j t", t=1024)[:, :, tk], writes=[VF])
            b1 = P.bank()
            for j in range(8):
                P.mm(b1[0:32, 0:128], v1b[:, j, :], XV[:, j, tk], j == 0, j == 7, [v1b, XV], [b1])
            P.cp("act", t1b[0:32, :], b1[0:32, 0:128], [b1], [t1b])
            bv = (P.bank(), P.bank())
            for j in range(8):
                P.mm(bv[j // 4][:, (j % 4) * 128:(j % 4 + 1) * 128], v2b[0:32, j * 128:(j + 1) * 128], t1b[0:32, :], True, True, [v2b, t1b], [bv[j // 4]])
            for j in range(8):
                P.act(gate[:, j, :], bv[j // 4][:, (j % 4) * 128:(j % 4 + 1) * 128], AF.Sigmoid, [bv[j // 4], lp], [gate], bias=lp[:, LP_V0 + j:LP_V0 + j + 1])
            P.tt("pool", T5[:], VF[:], v_in, ALU.subtract, [VF, XV], [T5])
            P.tt("pool", T5[:], T5[:], gate[:], ALU.mult, [T5, gate], [T5])
            P.tt("pool", Vf[:], T5[:], v_in, ALU.add, [T5, XV], [Vf])
        bb = P.bank()
        for j in range(8):
            P.mm(bb[:, 2 * j:2 * j + 2], rkb[:, j, :], K.hindb[:, 0:2], True, True, [rkb, K.hindb], [bb])
        P.cp("dve", rks[:], bb[:, 0:16], [bb], [rks])
        for n_, (src, dst) in enumerate(((At, A_tm), (Vf, V_tm), (Bh, Bh_tm), (Kh, Kh_tm), (Rt, R_tm))):
            bt = P.bank()
            btb = bt[:, :].bitcast(BF16)
            for j in range(8):
                P.tr(btb[:, j * 128:(j + 1) * 128], src[:, j, :], K.identb[:], [src, K.identb], [bt])
            P.cp("act" if n_ % 2 == 0 else "dve", dst[:].rearrange("p a b -> p (a b)"), btb[:, 0:1024], [bt], [dst])
        bo = bon[i % 2]
        P.tt("pool", bo[:], V_tm[:].rearrange("p j (h v) -> p (j h) v", v=64), rks[:].unsqueeze(2).broadcast_to([128, 16, 64]), ALU.mult, [V_tm, rks], [bo])
        P.dma(S["bonus_o"][:, i, :].bitcast(BF16), bo[:].rearrange("p a b -> p (a b)"), reads=[bo])

        for ci in range(2):
            combos = [(j, g, (g if ci == 0 else 1 - g)) for j in range(8) for g in range(2)]
            fm = lambda t, j, hl, p: t[64 * hl:64 * hl + 64, j, 64 * p:64 * p + 64]
            tmr = lambda t, j, g, hl: t[64 * g:64 * g + 64, j, 64 * hl:64 * hl + 64]
            bM = (P.bank(), P.bank())
            for (j, g, hl) in combos:
                P.mm(bM[j // 4][64 * g:64 * g + 64, (j % 4) * 128 + 64 * g:(j % 4) * 128 + 64 * g + 64], fm(Bt, j, hl, g), fm(At, j, hl, g), True, True, [Bt, At], [bM[j // 4]])
            bN = (P.bank(), P.bank())
            for (j, g, hl) in combos:
                P.mm(bN[j // 4][64 * g:64 * g + 64, (j % 4) * 128 + 64 * g:(j % 4) * 128 + 64 * g + 64], fm(At, j, hl, g), fm(Bt, j, hl, g), True, True, [Bt, At], [bN[j // 4]])
            for hb in range(2):
                v4 = lambda bk: bk[:, 0:512].rearrange("p (a b) -> p a b", b=128)
                P.tt("dve", Yc[0][:, hb * 4:(hb + 1) * 4, :], v4(bM[hb]), c[:, C_MSU:C_MSU + 128].unsqueeze(1).broadcast_to([128, 4, 128]), ALU.mult, [bM[hb], c], [Yc[0]])
                P.tt("dve", Xc[0][:, hb * 4:(hb + 1) * 4, :], v4(bN[hb]), c[:, C_MSL:C_MSL + 128].unsqueeze(1).broadcast_to([128, 4, 128]), ALU.mult, [bN[hb], c], [Xc[0]])
            P.tt("pool", Pt[:], Yc[0][:], c[:, C_ID:C_ID + 128].unsqueeze(1).broadcast_to([128, 8, 128]), ALU.add, [Yc[0], c], [Pt])
            bRb, bLak, bRk = P.bank(), P.bank(), P.bank()
            for (j, g, hl) in combos:
                o = lambda bk: bk[64 * g:64 * g + 64, j * 64:(j + 1) * 64]
                P.mm(o(bRb), fm(Bt, j, hl, g), fm(Rt, j, hl, g), True, True, [Bt, Rt], [bRb])
                P.mm(o(bLak), fm(Kt, j, hl, g), fm(At, j, hl, g), True, True, [Kt, At], [bLak])
                P.mm(o(bRk), fm(Kt, j, hl, g), fm(Rt, j, hl, g), True, True, [Kt, Rt], [bRk])
            v8 = lambda bk: bk[:, 0:512].rearrange("p (a b) -> p a b", b=64)
            iu = c[:, C_IU64:C_IU64 + 64].unsqueeze(1).broadcast_to([128, 8, 64])
            su = c[:, C_SU64:C_SU64 + 64].unsqueeze(1).broadcast_to([128, 8, 64])
            P.tt("dve", MrbT_s[:], v8(bRb), iu, ALU.mult, [bRb, c], [MrbT_s])
            P.tt("dve", LakT_s[:], v8(bLak), su, ALU.mult, [bLak, c], [LakT_s])
            P.tt("dve", MrkT_s[:], v8(bRk), iu, ALU.mult, [bRk, c], [MrkT_s])
            bY = P.bank()
            for (j, g, hl) in combos:
                P.mm(bY[64 * g:64 * g + 64, j * 64:(j + 1) * 64], LakT_s[64 * g:64 * g + 64, j, :], tmr(V_tm, j, g, hl), True, True, [LakT_s, V_tm], [bY])
            P.cp("act", AY[:, :, 64:128], v8(bY), [bY], [AY])
            for g in range(2):
                hl = g if ci == 0 else 1 - g
                P.cp("pool", AY[64 * g:64 * g + 64, :, 0:64], A_tm[64 * g:64 * g + 64, :, 64 * hl:64 * hl + 64], [A_tm], [AY])
            cur = 0
            for lv in range(1, 6):
                bXn = (P.bank(), P.bank())
                for j in range(8):
                    P.mm(bXn[j // 4][:, (j % 4) * 128:(j % 4 + 1) * 128], Yc[cur][:, j, :], Xc[cur][:, j, :], True, True, [Yc[cur], Xc[cur]], [bXn[j // 4]])
                if lv < 5:
                    bYn = (P.bank(), P.bank())
                    for j in range(8):
                        P.mm(bYn[j // 4][:, (j % 4) * 128:(j % 4 + 1) * 128], Xc[cur][:, j, :], Yc[cur][:, j, :], True, True, [Yc[cur], Xc[cur]], [bYn[j // 4]])
                nxt = 1 - cur
                for hb in range(2):
                    P.cp("act", Xc[nxt][:, hb * 4:(hb + 1) * 4, :], bXn[hb][:, 0:512].rearrange("p (a b) -> p a b", b=128), [bXn[hb]], [Xc[nxt]])
                    if lv < 5:
                        P.cp("dve", Yc[nxt][:, hb * 4:(hb + 1) * 4, :], bYn[hb][:, 0:512].rearrange("p (a b) -> p a b", b=128), [bYn[hb]], [Yc[nxt]])
                cur = nxt
                bPn = (P.bank(), P.bank())
                for j in range(8):
                    P.mm(bPn[j // 4][:, (j % 4) * 128:(j % 4 + 1) * 128], Xc[cur][:, j, :], Pt[:, j, :], True, True, [Xc[cur], Pt], [bPn[j // 4]])
                for hb in range(2):
                    P.tt("dve", Pt[:, hb * 4:(hb + 1) * 4, :], bPn[hb][:, 0:512].rearrange("p (a b) -> p a b", b=128), Pt[:, hb * 4:(hb + 1) * 4, :], ALU.add, [bPn[hb], Pt], [Pt])
            P.cp("act", TTb[:], Pt[:], [Pt], [TTb])
            bWU = (P.bank(), P.bank())
            for j in range(8):
                P.mm(bWU[j // 4][:, (j % 4) * 128:(j % 4 + 1) * 128], TTb[:, j, :], AY[:, j, :], True, True, [TTb, AY], [bWU[j // 4]])
            for hb in range(2):
                P.cp("act" if hb == 0 else "dve", WUs[:, hb * 4:(hb + 1) * 4, :], bWU[hb][:, 0:512].rearrange("p (a b) -> p a b", b=128), [bWU[hb]], [WUs])
            bLR, bOm, bD, bOl = P.bank(), P.bank(), P.bank(), P.bank()
            for (j, g, hl) in combos:
                Wt_ = WUs[64 * g:64 * g + 64, j, 0:64]
                Ut_ = WUs[64 * g:64 * g + 64, j, 64:128]
                of = lambda bk: bk[64 * hl:64 * hl + 64, j * 64:(j + 1) * 64]
                ot = lambda bk: bk[64 * g:64 * g + 64, j * 64:(j + 1) * 64]
                P.mm(of(bLR), Wt_, tmr(Bh_tm, j, g, hl), True, True, [WUs, Bh_tm], [bLR])
                P.mm(of(bOm), Wt_, MrbT_s[64 * g:64 * g + 64, j, :], True, False, [WUs, MrbT_s], [bOm])
                P.mm(of(bOm), tmr(R_tm, j, g, hl), K.identb[64 * g:64 * g + 64, 64 * g:64 * g + 64], False, True, [R_tm, K.identb], [bOm])
                P.mm(of(bD), tmr(Bh_tm, j, g, hl), Ut_, True, False, [Bh_tm, WUs], [bD])
                P.mm(of(bD), tmr(Kh_tm, j, g, hl), tmr(V_tm, j, g, hl), False, True, [Kh_tm, V_tm], [bD])
                P.mm(ot(bOl), MrbT_s[64 * g:64 * g + 64, j, :], Ut_, True, False, [MrbT_s, WUs], [bOl])
                P.mm(ot(bOl), MrkT_s[64 * g:64 * g + 64, j, :], tmr(V_tm, j, g, hl), False, True, [MrkT_s, V_tm], [bOl])
            P.cp("act", LRs[ci][:], v8(bLR), [bLR], [LRs[ci]])
            P.cp("dve", OmTs[ci][:], v8(bOm), [bOm], [OmTs[ci]])
            P.cp("act", Ds[ci][:], v8(bD), [bD], [Ds[ci]])
            P.cp("dve", Ols[ci][:], v8(bOl), [bOl], [Ols[ci]])

        omp = OmpTt[i % 2]
        for p in range(2):
            P.cp("act", Zb[:], Z[:], [Z], [Zb])
            bZ = (P.bank(), P.bank())
            bOp = P.bank()
            for j in range(8):
                for hl in range(2):
                    ci = 0 if hl == p else 1
                    rs = slice(64 * hl, 64 * hl + 64)
                    zo = bZ[j // 4][rs, (j % 4) * 128:(j % 4 + 1) * 128]
                    P.mm(zo, LRs[ci][rs, j, :], Zb[rs, j, :], True, False, [LRs[ci], Zb], [bZ[j // 4]])
                    P.mm(bZ[j // 4][rs, (j % 4) * 128:(j % 4) * 128 + 64], K.identb[rs, 64 * hl:64 * hl + 64], Ds[ci][rs, j, :], False, True, [K.identb, Ds[ci]], [bZ[j // 4]])
                    P.mm(bOd[ci][64 * p:64 * p + 64, j * 64:(j + 1) * 64], OmTs[ci][rs, j, :], Zb[rs, j, 0:64], True, True, [OmTs[ci], Zb], [bOd[ci]])
                    P.mm(bOp[rs, j * 64:(j + 1) * 64], Zb[rs, j, 64:128], OmTs[ci][rs, j, :], True, True, [Zb, OmTs[ci]], [bOp])
            P.cp("act", omp[:, :, 64 * p:64 * p + 64], bOp[:, 0:512].rearrange("p (a b) -> p a b", b=64), [bOp], [omp])
            P.tt("pool", Ztmp[:], Z[:], gamC[:, :, p:p + 1].broadcast_to([128, 8, 128]), ALU.mult, [Z, gamC], [Ztmp])
            for hb in range(2):
                P.tt("dve", Z[:, hb * 4:(hb + 1) * 4, :], bZ[hb][:, 0:512].rearrange("p (a b) -> p a b", b=128), Ztmp[:, hb * 4:(hb + 1) * 4, :], ALU.add, [bZ[hb], Ztmp], [Z])
        P.dma(S["ompt_o"][:, i, :].bitcast(BF16), omp[:].rearrange("p a b -> p (a b)"), reads=[omp])
        op_ = Opt[i % 2]
        opv = op_[:].rearrange("p (j h) v -> p j h v", h=2)
        for ci in range(2):
            for p in range(2):
                hl = p if ci == 0 else 1 - p
                rs = slice(64 * p, 64 * p + 64)
                P.tt("dve", opv[rs, :, hl, :], bOd[ci][rs, 0:512].rearrange("p (a b) -> p a b", b=64), Ols[ci][rs, :, :], ALU.add, [bOd[ci], Ols[ci]], [op_])
        P.dma(S["oprime_o"][:, i, :].bitcast(BF16), op_[:].rearrange("p a b -> p (a b)"), reads=[op_])
    P.dma(S["zq_o"], Z[:].rearrange("p a b -> p (a b)"), reads=[Z])
    P.nrot = 8

def layer_norm_tile(P, z, w_bc, b_bc, out, scr):
    st, mv, rs = scr["st"], scr["mv"], scr["rs"]
    for q in range(4):
        P.op("dve", lambda e, q=q: e.bn_stats(out=st[:, q, :], in_=z[:, q * 512:(q + 1) * 512]), reads=[z.b], writes=[st.b])
    P.op("dve", lambda e: e.bn_aggr(out=mv[:], in_=st[:].rearrange("p a b -> p (a b)")), reads=[st.b], writes=[mv.b])
    P.ts("dve", rs[:], mv[:, 1:2], LN_EPS, None, ALU.add, None, [mv], [rs])
    P.act(rs[:], rs[:], AF.Sqrt, [rs], [rs])
    P.op("dve", lambda e: e.reciprocal(out=rs[:], in_=rs[:]), reads=[rs.b], writes=[rs.b])
    P.ts("dve", z[:], z[:], mv[:, 0:1], rs[:, 0:1], ALU.subtract, ALU.mult, [z, mv, rs], [z])
    P.tt("pool", z[:], z[:], w_bc[:], ALU.mult, [z, w_bc], [z])
    P.tt("pool", out[:], z[:], b_bc[:], ALU.add, [z, b_bc], [out])


def build_p2(l):
    P = Prog(ARENA_WORDS)
    x_in = P.dram_in("x_in", [1024, 2048])
    modg = P.dram_in("modg", [192, 128])
    consts = P.dram_in("consts", [128, C_N])
    cflags = P.dram_in("cflags", [128, 16])
    yattn_i = P.dram_in("yattn", [128, 8, 512])
    oprime_i = P.dram_in("oprime", [128, 8, 512])
    ompt_i = P.dram_in("ompt", [128, 8, 512])
    bonus_i = P.dram_in("bonus", [128, 8, 512])
    sgd_i = P.dram_in("sgd", [128, 1024])
    zq_all = P.dram_in("zq_all", [8, 128, 1024])
    g2_in = P.dram_in("g2", [160, 1024])
    gnw_in = P.dram_in("gn_w", [1, 1024])
    gnb_in = P.dram_in("gn_b", [1, 1024])
    w_out = P.dram_in("w_out", [2048, 2048])
    ln1w_in = P.dram_in("ln1_w", [1, 2048])
    ln1b_in = P.dram_in("ln1_b", [1, 2048])
    wgu = P.dram_in("w_gate_up", [2048, 2 * FF])
    wdn = P.dram_in("w_down", [FF, 2048])
    ln2w_in = P.dram_in("ln2_w", [1, 2048])
    ln2b_in = P.dram_in("ln2_b", [1, 2048])
    x_out = P.dram_out("x_out", [1024, 2048])
    xmid = P.nc.dram_tensor("xmid_scr", [1024, 2048], F32, kind="Internal").ap()
    xmid_b = Buf("xmid")

    K = setup_consts(P, consts, cflags)
    modc = load_mod_cols(P, K, modg, l)
    sc1f = P.alloc([128, 16], F32, "sc1f")
    P.ts("dve", sc1f[:], modc[:, 64:80], 1.0, None, ALU.add, None, [modc], [sc1f])
    shf = Tl(modc[:, 48:64], modc.b)
    modrow = modg.rearrange("a b -> (a b)")
    def mod_bc(idx):
        off = (l * 96 + idx * 16) * 128
        return modrow[off:off + 2048].unsqueeze(0).broadcast_to([128, 2048])
    scr = dict(st=P.alloc([128, 4, 6], F32, "ln_st"), mv=P.alloc([128, 2], F32, "ln_mv"), rs=P.alloc([128, 1], F32, "ln_rs"))
    mk0 = P.mark()
    P.hi_mode = True

    yT_all = P.alloc([128, 16, 1024], BF16, "yT_all", hi=False)
    for tt in range(8):
        P.dma(yT_all[:, 0:8, tt * 128:(tt + 1) * 128], yattn_i[:, tt, :].bitcast(BF16).rearrange("p (c t) -> p c t", t=128), writes=[yT_all])
    mkR = P.mark()
    H0 = P.alloc([128, 16, 64], F32, "H0")
    P.memset("pool", H0[:], 0.0, [H0])
    ZQ = [P.alloc([128, 2, 8, 128], F32, f"ZQ{i}") for i in range(2)]
    PT = P.alloc([128, 16, 64], F32, "PT")
    Ht = P.alloc([128, 16, 64], F32, "Ht")
    for cidx in range(7):
        zq = ZQ[cidx % 2]
        P.dma(zq[0:64], zq_all[cidx].rearrange("(hl k) (j n) -> k hl j n", k=64, n=128), writes=[zq])
        bT = (P.bank(), P.bank())
        for j in range(8):
            for hl in range(2):
                hd = 2 * j + hl
                P.tr(bT[hd // 8][0:64, (hd % 8) * 64:(hd % 8 + 1) * 64], zq[0:64, hl, j, 64:128], K.ident[0:64, 0:64], [zq, K.c], [bT[hd // 8]])
        for hb in range(2):
            P.cp("act", PT[0:64, hb * 8:(hb + 1) * 8, :], bT[hb][0:64, 0:512].rearrange("p (a b) -> p a b", b=64), [bT[hb]], [PT])
        bH = (P.bank(), P.bank())
        for hd in range(16):
            P.mm(bH[hd // 8][0:64, (hd % 8) * 64:(hd % 8 + 1) * 64], PT[0:64, hd, :], H0[0:64, hd, :], True, True, [PT, H0], [bH[hd // 8]])
        Htv = Ht[0:64].rearrange("p (j hl) v -> p j hl v", hl=2)
        for hl in range(2):
            for hb in range(2):
                P.tt("dve", Htv[:, hb * 4:(hb + 1) * 4, hl, :],
                     bH[hb][0:64, 0:512].rearrange("p (j hl v) -> p j hl v", hl=2, v=64)[:, :, hl, :],
                     zq[0:64, hl, hb * 4:(hb + 1) * 4, 0:64], ALU.add, [bH[hb], zq], [Ht])
        P.tt("pool", Ht[0:64], Ht[0:64], H0[0:64], ALU.subtract, [Ht, H0], [Ht])
        P.stt(H0[0:64], Ht[0:64], K.fl[0:64, 1 + cidx:2 + cidx], H0[0:64], ALU.mult, ALU.add, [Ht, K.fl, H0], [H0])
    H0b = P.alloc([128, 16, 64], BF16, "H0b")
    P.cp("dve", H0b[0:64], H0[0:64], [H0], [H0b])
    HsF = P.alloc([128, 8, 64], BF16, "HsF")
    H0bv = H0b[0:64].rearrange("p (j hl) v -> p j hl v", hl=2)
    P.dma(HsF[0:64], H0bv[:, :, 0, :], reads=[H0b], writes=[HsF])
    P.dma(HsF[64:128], H0bv[:, :, 1, :], reads=[H0b], writes=[HsF])

    g2t = P.alloc([128, 2, 1024], BF16, "g2t")
    P.memset("pool", g2t[:, 1, :], 0.0, [g2t])
    P.dma(g2t[:, 0, :], g2_in[0:128, :], writes=[g2t], q="pool")
    P.dma(g2t[0:32, 1, :], g2_in[128:160, :], writes=[g2t], q="pool")
    SG = P.alloc([128, 2, 1024], BF16, "SG")
    P.dma(SG[:].rearrange("p a b -> p (a b)"), sgd_i.bitcast(BF16), writes=[SG])
    gnw = P.alloc([128, 16, 64], F32, "gnw")
    gnb = P.alloc([128, 16, 64], F32, "gnb")
    P.dma(gnw[:].rearrange("p a b -> p (a b)"), gnw_in.broadcast_to([128, 1024]), writes=[gnw])
    P.dma(gnb[:].rearrange("p a b -> p (a b)"), gnb_in.broadcast_to([128, 1024]), writes=[gnb])
    omp = [P.alloc([128, 8, 128], BF16, f"ompL{i}") for i in range(2)]
    opr = [P.alloc([128, 16, 64], BF16, f"oprL{i}") for i in range(2)]
    bon = [P.alloc([128, 16, 64], BF16, f"bonL{i}") for i in range(2)]
    o_t = P.alloc([128, 16, 64], F32, "o_t")
    sq = P.alloc([128, 16, 64], F32, "sq")
    s1 = P.alloc([128, 16], F32, "s1")
    s2 = P.alloc([128, 16], F32, "s2")
    yb = P.alloc([128, 16, 64], BF16, "yb")
    for i in range(8):
        om, op_, bo = omp[i % 2], opr[i % 2], bon[i % 2]
        P.dma(om[:].rearrange("p a b -> p (a b)"), ompt_i[:, i, :].bitcast(BF16), writes=[om])
        P.dma(op_[:].rearrange("p a b -> p (a b)"), oprime_i[:, i, :].bitcast(BF16), writes=[op_])
        P.dma(bo[:].rearrange("p a b -> p (a b)"), bonus_i[:, i, :].bitcast(BF16), writes=[bo], q="act")
        bX, bY = P.bank(), P.bank()
        for j in range(8):
            for hl in range(2):
                for p in range(2):
                    bk = bX if hl == p else bY
                    P.mm(bk[64 * p:64 * p + 64, j * 64:(j + 1) * 64], om[64 * hl:64 * hl + 64, j, 64 * p:64 * p + 64],
                         HsF[64 * hl:64 * hl + 64, j, :], True, True, [om, HsF], [bk])
        ov = o_t[:].rearrange("p (j hl) v -> p j hl v", hl=2)
        opv = op_[:].rearrange("p (j hl) v -> p j hl v", hl=2)
        for ci, bk in enumerate((bX, bY)):
            for p in range(2):
                hl = p if ci == 0 else 1 - p
                rs_ = slice(64 * p, 64 * p + 64)
                P.tt("dve", ov[rs_, :, hl, :], bk[rs_, 0:512].rearrange("p (a b) -> p a b", b=64), opv[rs_, :, hl, :], ALU.add, [bk, op_], [o_t])
        P.op("dve", lambda e: e.tensor_reduce(out=s1[:], in_=o_t[:], axis=AX.X, op=ALU.add), reads=[o_t.b], writes=[s1.b])
        P.tt("pool", sq[:], o_t[:], o_t[:], ALU.mult, [o_t], [sq])
        P.op("dve", lambda e: e.tensor_reduce(out=s2[:], in_=sq[:], axis=AX.X, op=ALU.add), reads=[sq.b], writes=[s2.b])
        P.ts("dve", s1[:], s1[:], 1.0 / 64, None, ALU.mult, None, [s1], [s1])
        P.tt("dve", sq[:, :, 0], s1[:], s1[:], ALU.mult, [s1], [sq])
        P.stt(s2[:], s2[:], 1.0 / 64, sq[:, :, 0], ALU.mult, ALU.subtract, [s2, sq], [s2])
        P.ts("dve", s2[:], s2[:], GN_EPS, None, ALU.add, None, [s2], [s2])
        P.act(s2[:], s2[:], AF.Sqrt, [s2], [s2])
        P.op("dve", lambda e: e.reciprocal(out=s2[:], in_=s2[:]), reads=[s2.b], writes=[s2.b])
        P.tt("dve", o_t[:], o_t[:], s1[:].unsqueeze(2).broadcast_to([128, 16, 64]), ALU.subtract, [o_t, s1], [o_t])
        P.tt("dve", o_t[:], o_t[:], s2[:].unsqueeze(2).broadcast_to([128, 16, 64]), ALU.mult, [o_t, s2], [o_t])
        P.tt("pool", o_t[:], o_t[:], gnw[:], ALU.mult, [o_t, gnw], [o_t])
        P.tt("pool", o_t[:], o_t[:], gnb[:], ALU.add, [o_t, gnb], [o_t])
        P.tt("pool", o_t[:], o_t[:], bo[:], ALU.add, [o_t, bo], [o_t])
        bg = (P.bank(), P.bank())
        for hb in range(2):
            for c2 in range(2):
                P.mm(bg[hb][:, 0:512], SG[:, c2, i * 128:(i + 1) * 128], g2t[:, c2, hb * 512:(hb + 1) * 512], c2 == 0, c2 == 1, [SG, g2t], [bg[hb]])
        for hb in range(2):
            P.tt("dve", yb[:, hb * 8:(hb + 1) * 8, :], bg[hb][:, 0:512].rearrange("p (a b) -> p a b", b=64), o_t[:, hb * 8:(hb + 1) * 8, :], ALU.mult, [bg[hb], o_t], [yb])
        bt = P.bank()
        btb = bt[:, :].bitcast(BF16)
        ybf = yb[:].rearrange("p a b -> p (a b)")
        for j in range(8):
            P.tr(btb[:, j * 128:(j + 1) * 128], ybf[:, j * 128:(j + 1) * 128], K.identb[:], [yb, K.identb], [bt])
        P.cp("act", yT_all[:, 8:16, i * 128:(i + 1) * 128], btb[:, 0:1024].rearrange("p (a b) -> p a b", b=128), [bt], [yT_all])
    P.release(mkR)

    wo = P.alloc([128, 16, 2048], BF16, "wo")
    wo_v = w_out.rearrange("(kc p) n -> p kc n", p=128)
    for q in range(4):
        P.dma(wo[:, :, q * 512:(q + 1) * 512], wo_v[:, :, q * 512:(q + 1) * 512], writes=[wo], q="pool")
    gt_bc = P.alloc([128, 2048], F32, "gt_bc")
    lw_bc = P.alloc([128, 2048], F32, "lw_bc")
    lb_bc = P.alloc([128, 2048], F32, "lb_bc")
    P.dma(gt_bc[:], mod_bc(2), writes=[gt_bc])
    P.dma(lw_bc[:], ln1w_in.broadcast_to([128, 2048]), writes=[lw_bc])
    P.dma(lb_bc[:], ln1b_in.broadcast_to([128, 2048]), writes=[lb_bc])
    xt = [P.alloc([128, 2048], F32, f"xt{i}") for i in range(2)]
    zt = [P.alloc([128, 2048], F32, f"zt{i}") for i in range(2)]
    for tt in range(8):
        xb, z = xt[tt % 2], zt[tt % 2]
        P.dma(xb[:], x_in[tt * 128:(tt + 1) * 128, :], writes=[xb], q="act")
        for cb in range(4):
            bk = P.bank()
            for ch in range(16):
                P.mm(bk[:, 0:512], yT_all[:, ch, tt * 128:(tt + 1) * 128], wo[:, ch, cb * 512:(cb + 1) * 512], ch == 0, ch == 15, [yT_all, wo], [bk])
            P.tt("dve", z[:, cb * 512:(cb + 1) * 512], bk[:, 0:512], gt_bc[:, cb * 512:(cb + 1) * 512], ALU.mult, [bk, gt_bc], [z])
        P.stt(z[:], xb[:], ALPHA, z[:], ALU.mult, ALU.add, [xb, z], [z])
        layer_norm_tile(P, z, lw_bc, lb_bc, z, scr)
        P.dma(xmid[tt * 128:(tt + 1) * 128, :], z[:], reads=[z], writes=[xmid_b])
    P.release(mk0)

    P.hi_mode = False
    gtf_bc = P.alloc([128, 2048], F32, "gtf_bc")
    l2w_bc = P.alloc([128, 2048], F32, "l2w_bc")
    l2b_bc = P.alloc([128, 2048], F32, "l2b_bc")
    P.dma(gtf_bc[:], mod_bc(5), writes=[gtf_bc])
    P.dma(l2w_bc[:], ln2w_in.broadcast_to([128, 2048]), writes=[l2w_bc])
    P.dma(l2b_bc[:], ln2b_in.broadcast_to([128, 2048]), writes=[l2b_bc])
    u2T = P.alloc([128, KC, 512], BF16, "u2T")
    hT = P.alloc([128, FFC, 512], BF16, "hT")
    z2 = P.alloc([128, 4, 2048], F32, "z2")
    xt2 = [P.alloc([128, 2048], F32, "xt2_0")]
    sg = [P.alloc([128, 512], F32, f"sg{i}") for i in range(2)]
    Wg = WStream(P, KC, 128, nbuf=2, name="wg")
    Wu = WStream(P, KC, 128, nbuf=2, name="wu")
    Wd = WStream(P, 11, 512, nbuf=2, name="wd")
    wgu_v = wgu.rearrange("(kc p) n -> p kc n", p=128)
    wdn_v = wdn.rearrange("(j p) n -> p j n", p=128)
    for hf in range(2):
        srcs = [xmid[(hf * 4 + t) * 128:(hf * 4 + t + 1) * 128, :] for t in range(4)]
        for t, src in enumerate(srcs):
            xb = xt2[0]
            P.dma(xb[:], src, reads=[xmid_b], writes=[xb])
            for f4 in range(4):
                bk = P.bank()
                for q in range(4):
                    fc = f4 * 4 + q
                    P.tr(bk[:, q * 128:(q + 1) * 128], xb[:, fc * 128:(fc + 1) * 128], K.ident, [xb, K.c], [bk])
                for q in range(4):
                    fc = f4 * 4 + q
                    o = u2T[:, fc, t * 128:(t + 1) * 128]
                    i_ = bk[:, q * 128:(q + 1) * 128]
                    if q % 2 == 0:
                        P.ts("dve", o, i_, sc1f[:, fc:fc + 1], shf[:, fc:fc + 1], ALU.mult, ALU.add, [bk, sc1f, shf], [u2T])
                    else:
                        P.act(o, i_, AF.Identity, [bk, sc1f, shf], [u2T], bias=shf[:, fc:fc + 1], scale=sc1f[:, fc:fc + 1])
        for j in range(FFC):
            wg = Wg.load(wgu_v, j * 128, 128)
            wu = Wu.load(wgu_v, FF + j * 128, 128)
            bg_, bu_ = P.bank(), P.bank()
            for kc in range(KC):
                P.mm(bg_[:, 0:512], wg[:, kc, :], u2T[:, kc, :], kc == 0, kc == KC - 1, [wg, u2T], [bg_])
            for kc in range(KC):
                P.mm(bu_[:, 0:512], wu[:, kc, :], u2T[:, kc, :], kc == 0, kc == KC - 1, [wu, u2T], [bu_])
            s_ = sg[j % 2]
            P.act(s_[:], bg_[:, 0:512], AF.Silu, [bg_], [s_])
            P.tt("dve", hT[:, j, :], bu_[:, 0:512], s_[:], ALU.mult, [bu_, s_], [hT])
        for cb in range(4):
            bks = [P.bank() for _ in range(4)]
            for jb in range(4):
                wd = Wd.load(wdn_v, cb * 512, 512, kc0=jb * 11, nkc=11)
                for jj in range(11):
                    j = jb * 11 + jj
                    for t in range(4):
                        P.mm(bks[t][:, 0:512], hT[:, j, t * 128:(t + 1) * 128], wd[:, jj, :], j == 0, j == FFC - 1, [hT, wd], [bks[t]])
            for t in range(4):
                P.tt("dve", z2[:, t, cb * 512:(cb + 1) * 512], bks[t][:, 0:512], gtf_bc[:, cb * 512:(cb + 1) * 512], ALU.mult, [bks[t], gtf_bc], [z2])
        for t in range(4):
            xb = xt2[0]
            P.dma(xb[:], srcs[t], reads=[xmid_b], writes=[xb])
            zt_ = Tl(z2[:, t, :], z2.b)
            P.stt(zt_[:], xb[:], ALPHA, zt_[:], ALU.mult, ALU.add, [xb, z2], [z2])
            layer_norm_tile(P, zt_, l2w_bc, l2b_bc, zt_, scr)
            P.dma(x_out[(hf * 4 + t) * 128:(hf * 4 + t + 1) * 128, :], zt_[:], reads=[z2])
    return P


def _bf16_words(a):
    return a


def kernel(**inp):
    inp = {k: np.asarray(v) for k, v in inp.items()}
    consts = build_consts()
    cores = list(range(NCORE))
    ncm = build_mod_program()
    c_col = col_layout(inp["c"][0])
    maps = []
    for i in cores:
        l, cs = i // 4, (i % 4) * 3072
        maps.append({"c_col": c_col, "wa": np.ascontiguousarray(inp["w_ada"][l][:, cs:cs + 3072]),
                     "ba_col": col_layout(inp["b_ada"][l][cs:cs + 3072]), "consts": consts})
    res = run_bass_kernel_spmd(ncm, maps, core_ids=cores)
    modg = np.ascontiguousarray(np.concatenate([r["mod_out"] for r in res.results], axis=0))
    x_cur = np.ascontiguousarray(inp["x"][0])
    pos = inp["positions"][0].astype(np.int32)
    vfirst = [None] * NCORE
    for l in range(2):
        P1, S = build_p1(l)
        rwkv_phase_b(P1, S)
        nc1 = P1.emit()
        lp = build_lp(inp, l)
        maps = []
        for i in cores:
            fl = np.zeros((128, 16), np.float32)
            fl[:, 0] = 1.0 if i > 0 else 0.0
            for c in range(8):
                fl[:, 1 + c] = 1.0 if c < i else 0.0
            halo = x_cur[i * 1024 - 128:i * 1024] if i > 0 else np.zeros((128, 2048), np.float32)
            p_ = np.empty((1, 1152), np.int32)
            p_[0, 128:] = pos[i * 1024:(i + 1) * 1024]
            p_[0, :128] = pos[i * 1024 - 128:i * 1024] if i > 0 else pos[0]
            m = {"x_in": x_cur[i * 1024:(i + 1) * 1024], "x_halo": np.ascontiguousarray(halo), "pos": p_, "modg": modg,
                 "consts": consts, "cflags": fl, "lp": lp, "w_in": inp["w_in"][l], "sinks": inp["attn_sinks"][l][None, :],
                 "w2": inp["rwkv_w2"][l], "a2": inp["rwkv_a2"][l]}
            if l > 0:
                m.update({"v1": inp["rwkv_v1"][0], "v2": inp["rwkv_v2"][0], "vfirst": vfirst[i]})
            maps.append(m)
        r1 = run_bass_kernel_spmd(nc1, maps, core_ids=cores).results
        if l == 0:
            vfirst = [r["vfirst_o"] for r in r1]
        zq_all = np.ascontiguousarray(np.stack([r["zq"] for r in r1], axis=0))
        nc2 = build_p2(l).emit()
        maps = []
        for i in cores:
            fl = maps and None
            fl = np.zeros((128, 16), np.float32)
            fl[:, 0] = 1.0 if i > 0 else 0.0
            for c in range(8):
                fl[:, 1 + c] = 1.0 if c < i else 0.0
            maps.append({"x_in": x_cur[i * 1024:(i + 1) * 1024], "modg": modg, "consts": consts, "cflags": fl,
                         "yattn": r1[i]["yattn"], "oprime": r1[i]["oprime"], "ompt": r1[i]["ompt"], "bonus": r1[i]["bonus"],
                         "sgd": r1[i]["sgd"], "zq_all": zq_all, "g2": inp["rwkv_g2"][l],
                         "gn_w": inp["rwkv_gn_w"][l][None, :], "gn_b": inp["rwkv_gn_b"][l][None, :],
                         "w_out": inp["w_out"][l], "ln1_w": inp["ln1_w"][l][None, :], "ln1_b": inp["ln1_b"][l][None, :],
                         "w_gate_up": inp["w_gate_up"][l], "w_down": inp["w_down"][l],
                         "ln2_w": inp["ln2_w"][l][None, :], "ln2_b": inp["ln2_b"][l][None, :]})
        r2 = run_bass_kernel_spmd(nc2, maps, core_ids=cores).results
        x_cur = np.ascontiguousarray(np.concatenate([r["x_out"] for r in r2], axis=0))
    return x_cur[None].astype(np.float32)
```
